# Optimizing a Trainium2 kernel written in Bass

```python
import jax, jax.numpy as jnp
from jax import lax
import numpy as np

D_MODEL = 2048
BATCH = 1
SEQ = 16384
DEPTH = 2

N_META = 16
N_A_LAYERS = DEPTH // 2
N_B_LAYERS = DEPTH - N_A_LAYERS
GLA_HEADS = 4
GLA_QK = D_MODEL // 2
GLA_VD = D_MODEL
GLA_DK = GLA_QK // GLA_HEADS
GLA_DV = GLA_VD // GLA_HEADS
GLA_RANK = 16
GLA_GATE_NORM = 16.0
GLA_CHUNK = 64
GLA_IN = 2 * GLA_QK + 2 * GLA_VD + GLA_RANK
SWA_HEAD_DIM = 64
SWA_Q_HEADS = D_MODEL // SWA_HEAD_DIM
SWA_KV_HEADS = 4
SWA_GROUP = SWA_Q_HEADS // SWA_KV_HEADS
SWA_WINDOW = 128
SWA_BLOCK = 128
ROPE_THETA = 10000.0
D_FF = 4 * D_MODEL
RMS_EPS = 1e-6
NEG_INF = -1e30

kernel_name = 'hybrid_gla_swa_sink_yoco_meta'


def rmsnorm(x, g):
    xf = x.astype(jnp.float32)
    y = xf * lax.rsqrt(jnp.mean(xf * xf, axis=-1, keepdims=True) + RMS_EPS)
    return (y * g.astype(jnp.float32)).astype(x.dtype)


def rope(x, pos):
    half = x.shape[-1] // 2
    inv_freq = ROPE_THETA ** (-jnp.arange(half, dtype=jnp.float32) / half)
    ang = pos.astype(jnp.float32)[:, None] * inv_freq[None, :]
    cos = jnp.cos(ang)[None, :, None, :].astype(x.dtype)
    sin = jnp.sin(ang)[None, :, None, :].astype(x.dtype)
    x1, x2 = x[..., :half], x[..., half:]
    return jnp.concatenate([x1 * cos - x2 * sin, x2 * cos + x1 * sin], axis=-1)


def sq_relu_mlp(xn, w_up, w_down):
    return jnp.square(jax.nn.relu(xn @ w_up)) @ w_down


def gla_mixer(xn, w_in, w_gate_up, b_gate, norm_out, w_out):
    B, L, _ = xn.shape
    pad = GLA_CHUNK - N_META
    nc = (L + pad) // GLA_CHUNK
    q, k, v, g, r = jnp.split(xn @ w_in, [GLA_QK, 2 * GLA_QK, 2 * GLA_QK + GLA_VD, 2 * GLA_QK + 2 * GLA_VD], axis=-1)
    gk = jax.nn.log_sigmoid((r @ w_gate_up + b_gate).astype(jnp.float32)) / GLA_GATE_NORM

    def chunks(t, d):
        t = jnp.pad(t, ((0, 0), (pad, 0), (0, 0)))
        return t.reshape(B, nc, GLA_CHUNK, GLA_HEADS, d).transpose(1, 0, 3, 2, 4)

    qc = chunks(q * (GLA_DK ** -0.5), GLA_DK)
    kc = chunks(k, GLA_DK)
    vc = chunks(v, GLA_DV)
    gc = chunks(gk, GLA_DK)
    causal = jnp.tril(jnp.ones((GLA_CHUNK, GLA_CHUNK), dtype=bool))[:, :, None]

    def step(S, inp):
        q_c, k_c, v_c, g_c = inp
        b = jnp.cumsum(g_c, axis=2)
        diff = b[:, :, :, None, :] - b[:, :, None, :, :]
        decay = jnp.exp(jnp.where(causal, diff, -jnp.inf))
        attn = jnp.einsum('bhid,bhjd,bhijd->bhij', q_c, k_c, decay)
        o_intra = jnp.einsum('bhij,bhjv->bhiv', attn, v_c)
        o_inter = jnp.einsum('bhid,bhdv->bhiv', q_c * jnp.exp(b), S)
        b_last = b[:, :, -1:, :]
        k_dec = k_c * jnp.exp(b_last - b)
        S_new = S * jnp.exp(b_last[:, :, 0, :])[..., None] + jnp.einsum('bhjd,bhjv->bhdv', k_dec, v_c)
        return S_new, o_intra + o_inter

    S0 = jnp.zeros((B, GLA_HEADS, GLA_DK, GLA_DV), jnp.float32)
    _, o = lax.scan(step, S0, (qc, kc, vc, gc))
    o = o.transpose(1, 0, 3, 2, 4).reshape(B, nc * GLA_CHUNK, GLA_HEADS, GLA_DV)[:, pad:]
    o = o.astype(jnp.float32)
    o = o * lax.rsqrt(jnp.mean(o * o, axis=-1, keepdims=True) + RMS_EPS) * norm_out.astype(jnp.float32)
    o = o * jax.nn.silu(g.reshape(B, L, GLA_HEADS, GLA_DV).astype(jnp.float32))
    return o.reshape(B, L, GLA_VD).astype(xn.dtype) @ w_out


def shared_kv(h, kv_norm, w_kv, pos):
    B, L, _ = h.shape
    k, v = jnp.split(rmsnorm(h, kv_norm) @ w_kv, 2, axis=-1)
    k = rope(k.reshape(B, L, SWA_KV_HEADS, SWA_HEAD_DIM), pos)
    v = v.reshape(B, L, SWA_KV_HEADS, SWA_HEAD_DIM)
    return k, v


def swa_mixer(xn, k_sh, v_sh, w_q, sinks, w_out, pos):
    B, L, _ = xn.shape
    q_pad = SWA_BLOCK - N_META
    k_pad = q_pad + SWA_BLOCK
    nb = (L + q_pad) // SWA_BLOCK
    q = rope((xn @ w_q).reshape(B, L, SWA_Q_HEADS, SWA_HEAD_DIM), pos) * (SWA_HEAD_DIM ** -0.5)
    qb = jnp.pad(q, ((0, 0), (q_pad, 0), (0, 0), (0, 0)))
    qb = qb.reshape(B, nb, SWA_BLOCK, SWA_KV_HEADS, SWA_GROUP, SWA_HEAD_DIM).transpose(1, 0, 3, 4, 2, 5)

    def band(t):
        tp = jnp.pad(t, ((0, 0), (k_pad, 0), (0, 0), (0, 0)))
        tb = tp.reshape(B, nb + 1, SWA_BLOCK, SWA_KV_HEADS, SWA_HEAD_DIM)
        tb = jnp.concatenate([tb[:, :-1], tb[:, 1:]], axis=2)
        return tb.transpose(1, 0, 3, 2, 4)

    kb, vb = band(k_sh), band(v_sh)
    k_meta = k_sh[:, :N_META].transpose(0, 2, 1, 3)
    v_meta = v_sh[:, :N_META].transpose(0, 2, 1, 3)
    sink = sinks.astype(jnp.float32).reshape(1, SWA_KV_HEADS, SWA_GROUP, 1, 1)
    i_idx = jnp.arange(SWA_BLOCK)
    j_idx = jnp.arange(2 * SWA_BLOCK)
    m_idx = jnp.arange(N_META)

    def block(inp):
        n, q_n, k_n, v_n = inp
        q_pos = n * SWA_BLOCK - q_pad + i_idx
        k_pos = n * SWA_BLOCK - k_pad + j_idx
        rel = q_pos[:, None] - k_pos[None, :]
        band_ok = (k_pos[None, :] >= N_META) & (rel >= 0) & (rel < SWA_WINDOW)
        meta_ok = m_idx[None, :] <= q_pos[:, None]
        ok = jnp.concatenate([meta_ok, band_ok], axis=1)
        keys = jnp.concatenate([k_meta, k_n], axis=2)
        vals = jnp.concatenate([v_meta, v_n], axis=2)
        s = jnp.einsum('bkgid,bkjd->bkgij', q_n, keys).astype(jnp.float32)
        s = jnp.where(ok, s, NEG_INF)
        s = jnp.concatenate([s, jnp.broadcast_to(sink, s.shape[:-1] + (1,))], axis=-1)
        p = jax.nn.softmax(s, axis=-1)[..., :-1].astype(vals.dtype)
        return jnp.einsum('bkgij,bkjd->bkgid', p, vals)

    o = lax.map(block, (jnp.arange(nb), qb, kb, vb))
    o = o.transpose(1, 0, 4, 2, 3, 5).reshape(B, nb * SWA_BLOCK, SWA_Q_HEADS * SWA_HEAD_DIM)[:, q_pad:]
    return o @ w_out


def setup_inputs(seed: int = 0) -> dict:
    key = jax.random.key(seed)
    ks = jax.random.split(key, 18)
    f32 = jnp.float32

    def w(k, shape, fan_in):
        return jax.random.normal(k, shape, f32) * (fan_in ** -0.5)

    def gain(k, shape):
        return 1.0 + 0.05 * jax.random.normal(k, shape, f32)

    return {
        'x': jax.random.normal(ks[0], (BATCH, SEQ, D_MODEL), f32),
        'meta_tokens': jax.random.normal(ks[1], (N_META, D_MODEL), f32),
        'norm_mix': gain(ks[2], (DEPTH, D_MODEL)),
        'norm_mlp': gain(ks[3], (DEPTH, D_MODEL)),
        'w_mlp_up': w(ks[4], (DEPTH, D_MODEL, D_FF), D_MODEL),
        'w_mlp_down': w(ks[5], (DEPTH, D_FF, D_MODEL), D_FF),
        'a_w_in': w(ks[6], (N_A_LAYERS, D_MODEL, GLA_IN), D_MODEL),
        'a_w_gate_up': w(ks[7], (N_A_LAYERS, GLA_RANK, GLA_QK), GLA_RANK),
        'a_b_gate': 0.1 * jax.random.normal(ks[8], (N_A_LAYERS, GLA_QK), f32),
        'a_norm_out': gain(ks[9], (N_A_LAYERS, GLA_DV)),
        'a_w_out': w(ks[10], (N_A_LAYERS, GLA_VD, D_MODEL), GLA_VD),
        'kv_norm': gain(ks[11], (D_MODEL,)),
        'w_kv': w(ks[12], (D_MODEL, 2 * SWA_KV_HEADS * SWA_HEAD_DIM), D_MODEL),
        'b_w_q': w(ks[13], (N_B_LAYERS, D_MODEL, SWA_Q_HEADS * SWA_HEAD_DIM), D_MODEL),
        'b_sinks': 0.5 * jax.random.normal(ks[14], (N_B_LAYERS, SWA_Q_HEADS), f32),
        'b_w_out': w(ks[15], (N_B_LAYERS, SWA_Q_HEADS * SWA_HEAD_DIM, D_MODEL), SWA_Q_HEADS * SWA_HEAD_DIM),
        'norm_final': gain(ks[16], (D_MODEL,)),
    }


def reference(x, meta_tokens, norm_mix, norm_mlp, w_mlp_up, w_mlp_down, a_w_in, a_w_gate_up, a_b_gate,
              a_norm_out, a_w_out, kv_norm, w_kv, b_w_q, b_sinks, b_w_out, norm_final):
    B = x.shape[0]
    meta = jnp.broadcast_to(meta_tokens[None].astype(x.dtype), (B, N_META, D_MODEL))
    h = jnp.concatenate([meta, x], axis=1)
    pos = jnp.arange(h.shape[1], dtype=jnp.int32)
    k_sh, v_sh = None, None
    for layer in range(DEPTH):
        xn = rmsnorm(h, norm_mix[layer])
        if layer < N_A_LAYERS:
            i = layer
            h = h + gla_mixer(xn, a_w_in[i], a_w_gate_up[i], a_b_gate[i], a_norm_out[i], a_w_out[i])
        else:
            j = layer - N_A_LAYERS
            h = h + swa_mixer(xn, k_sh, v_sh, b_w_q[j], b_sinks[j], b_w_out[j], pos)
        h = h + sq_relu_mlp(rmsnorm(h, norm_mlp[layer]), w_mlp_up[layer], w_mlp_down[layer])
        if layer == N_A_LAYERS - 1:
            k_sh, v_sh = shared_kv(h, kv_norm, w_kv, pos)
    return rmsnorm(h[:, N_META:], norm_final)
```

```python
import numpy as np
import ml_dtypes
from contextlib import ExitStack

import concourse.bass as bass
import concourse.mybir as mybir
from concourse.bass_utils import run_bass_kernel_spmd

F32 = mybir.dt.float32
BF16 = mybir.dt.bfloat16
AF = mybir.ActivationFunctionType
ALU = mybir.AluOpType

NCORES = 8
D = 2048
SEQ = 16384
NMETA = 16
DFF = 8192
GIN = 6160
EPS = 1e-6
TOK_PER_CORE = SEQ // NCORES
NT_OWN = TOK_PER_CORE // 128
ENGS = ("pe", "act", "dve", "pool", "sp")


class _Op:
    __slots__ = ("eng", "fn", "deps", "signal", "evt", "waits", "dma_key")


class Sched:
    EPOCH = 12000

    def __init__(self):
        self.streams = {e: [] for e in ENGS}
        self.last_w = {}
        self.readers = {}

    def add(self, eng, fn, reads=(), writes=(), dma=None):
        op = _Op()
        op.eng, op.fn, op.dma_key, op.signal, op.evt, op.waits = eng, fn, dma, False, None, ()
        deps = []
        for k in reads:
            w = self.last_w.get(k)
            if w is not None:
                deps.append(w)
        for k in writes:
            w = self.last_w.get(k)
            if w is not None:
                deps.append(w)
            deps.extend(self.readers.get(k, ()))
        op.deps = deps
        for k in reads:
            self.readers.setdefault(k, []).append(op)
        for k in writes:
            self.last_w[k] = op
            self.readers[k] = []
        self.streams[eng].append(op)
        return op

    def fence(self, eng, fn, reads, writes):
        return self.add(eng, fn, reads=reads, writes=writes)

    @staticmethod
    def _skip(d, op):
        return d is op or (d.dma_key is None and op.dma_key is None and d.eng == "pe" and op.eng == "pe")

    def finalize(self, nc, stack):
        for e in ENGS:
            for op in self.streams[e]:
                for d in op.deps:
                    if not self._skip(d, op):
                        d.signal = True
        dma_sems = {}
        nsem = 0
        for e in ENGS:
            cnt = 0
            sems = []
            for op in self.streams[e]:
                if op.dma_key is not None:
                    s = dma_sems.get(op.dma_key)
                    if s is None:
                        s = [stack.enter_context(nc.semaphore("d%d" % nsem)), 0]
                        nsem += 1
                        dma_sems[op.dma_key] = s
                    s[1] += 16
                    op.evt = (s[0], s[1], 16)
                elif op.signal:
                    ep, v = divmod(cnt, self.EPOCH)
                    if ep >= len(sems):
                        sems.append(stack.enter_context(nc.semaphore("e%s%d" % (e, ep))))
                        nsem += 1
                    op.evt = (sems[ep], v + 1, 1)
                    cnt += 1
        for e in ENGS:
            seen = {}
            for op in self.streams[e]:
                need = {}
                for d in op.deps:
                    if self._skip(d, op):
                        continue
                    sem, val, _ = d.evt
                    k = id(sem)
                    if k not in need or need[k][1] < val:
                        need[k] = (sem, val)
                waits = []
                for k, (sem, val) in need.items():
                    if seen.get(k, 0) >= val:
                        continue
                    seen[k] = val
                    waits.append((sem, val))
                op.waits = waits
        self.nsem = nsem

    def emit(self, nc):
        with nc.Block() as block:
            decos = {"sp": block.sync, "act": block.scalar, "dve": block.vector,
                     "pe": block.tensor, "pool": block.gpsimd}
            for e in ENGS:
                stream = self.streams[e]

                def body(eng, stream=stream):
                    for op in stream:
                        for sem, val in op.waits:
                            eng.wait_ge(sem, val)
                        inst = op.fn(eng)
                        if op.evt is not None:
                            inst.then_inc(op.evt[0], op.evt[2])

                decos[e](body)


def _supertiles(ntiles):
    return [list(range(i, min(i + 4, ntiles))) for i in range(0, ntiles, 4)]


class Builder:
    def __init__(self, mode, nt_own=NT_OWN, nslot=4, stop=99):
        self.stop = stop
        self.mode = mode
        self.nt_own = nt_own
        self.nslot = nslot
        self.S = Sched()
        self.nc = bass.Bass("TRN2", target_bir_lowering=False)
        self.stack = ExitStack()
        self.fcount = 0

    def dram_in(self, name, shape, dt=F32):
        return self.nc.dram_tensor(name, list(shape), dt, kind="ExternalInput").ap()

    def dram_out(self, name, shape, dt=F32):
        return self.nc.dram_tensor(name, list(shape), dt, kind="ExternalOutput").ap()

    def sb(self, name, shape, dt):
        return self.stack.enter_context(self.nc.sbuf_tensor(name, list(shape), dt))

    def ps(self, name, shape, dt):
        return self.stack.enter_context(self.nc.psum_tensor(name, list(shape), dt))

    def op(self, eng, fn, reads=(), writes=()):
        return self.S.add(eng, fn, reads, writes)

    def dma(self, q, out, in_, reads=(), writes=(), key=None, **kw):
        if key is None:
            key = writes[0]
        return self.S.add(q, lambda e: e.dma_start(out=out, in_=in_, **kw), reads, writes, dma=key)

    def fence(self, reads, writes):
        sc = self.fence_sc
        i = self.fcount
        self.fcount += 1
        return self.S.add("dve", lambda e: e.memset(sc[:, (i % 8):(i % 8) + 1], 0.0), reads, list(writes) + [("fsc", i % 8)])

    def wstream_init(self, sched_list):
        self.w_list = sched_list
        self.w_loaded = 0
        self.w_next = 0
        self.w_released = 0
        self.wbuf = self.sb("wbuf", [128, self.nslot, 8, 512], BF16)

    def _wload(self):
        while self.w_loaded < min(self.w_released + self.nslot, len(self.w_list)):
            j = self.w_loaded
            slot = j % self.nslot
            self.dma("pool", self.wbuf[:, slot, :, :], self.w_list[j], reads=(), writes=[("w", slot)])
            self.w_loaded += 1

    def wget(self):
        k = self.w_next
        self.w_next += 1
        self._wload()
        assert k < self.w_loaded, "weight stream: too many outstanding blocks"
        slot = k % self.nslot
        return self.wbuf[:, slot, :, :], ("w", slot)

    def wrel(self, n):
        self.w_released += n
        assert self.w_released <= self.w_next
        self._wload()

    def wget_block(self):
        a = self.wget()
        b = self.wget()
        return [a, b]

    def build(self):
        nc, S = self.nc, self.S
        mode = self.mode
        nto = self.nt_own
        p2 = mode == "p2"
        ntiles = nto + 2 if p2 else nto
        sts = _supertiles(ntiles)
        ntok = ntiles * 128

        xin = self.dram_in("xin", [ntok, D])
        vmask = self.dram_in("vmask", [1, ntok])
        w_in = self.dram_in("w_in", [D, GIN])
        gamT = self.dram_in("gamT", [128, 5, 16])
        wg_d = self.dram_in("wg", [16, 1024])
        bg_d = self.dram_in("bgT", [128, 8])
        c_ident = self.dram_in("c_ident", [128, 128], BF16)
        c_reset = self.dram_in("c_reset", [128, 512])
        if p2:
            w_up = self.dram_in("w_up", [2, D, DFF])
            w_down = self.dram_in("w_down", [2, DFF, D])
            a_w_out = self.dram_in("a_w_out", [D, D])
            w_kv = self.dram_in("w_kv", [D, 512])
            b_w_q = self.dram_in("b_w_q", [D, D])
            b_w_out = self.dram_in("b_w_out", [D, D])
            nout_d = self.dram_in("nout", [1, 512])
            gfin_d = self.dram_in("gfin", [1, D])
            sinks_d = self.dram_in("sinks", [1, 32])
            c_cmask = self.dram_in("c_cmask", [128, 128], BF16)
            c_mb = self.dram_in("c_mb", [3, 128, 512], BF16)
            c_perm = self.dram_in("c_perm", [5, 128, 128])
            cos_d = self.dram_in("cosT", [128, ntok])
            sin_d = self.dram_in("sinT", [128, ntok])
            L_all = self.dram_in("L_all", [NCORES, 128, 8, 512])
            B_all = self.dram_in("B_all", [NCORES, 128, 8])
            cvalid = self.dram_in("cvalid", [128, NCORES + 1])
            out_d = self.dram_out("out", [nto * 128, D])
        else:
            L_out = self.dram_out("L_out", [128, 8, 512])
            B_out = self.dram_out("B_out", [128, 8])

        sb = self.sb
        ident = sb("ident", [128, 128], BF16)
        resetm = sb("resetm", [128, 512], F32)
        gam = sb("gam", [128, 5, 16], F32)
        wg32 = sb("wg32", [16, 1024], F32)
        nbg = sb("nbg", [128, 8], F32)
        wr_bf = sb("wr_bf", [128, 16, 16], BF16)
        self.fence_sc = sb("fsc", [128, 8], F32)
        small = sb("small", [128, 64], F32)
        h = sb("h", [128, 4, D], F32)
        xnT = sb("xnT", [128, 16, 512], BF16)
        xs = sb("xs", [128, 1, D], BF16)
        regA = sb("regA", [128, 20480], BF16)
        regB = sb("regB", [128, 6656], F32)
        Sst = sb("Sst", [128, 8, 512], F32)
        S_bf = sb("S_bf", [128, 8, 512], BF16)
        vm_bc = sb("vm_bc", [128, 512], F32)
        eblast = sb("eblast", [128, 8, 4], F32)
        bsum = sb("bsum", [128, 8], F32)
        qT = regA[:, 0:4096].rearrange("p (c t) -> p c t", c=8)
        kT = regA[:, 4096:8192].rearrange("p (c t) -> p c t", c=8)
        kdec = regA[:, 8192:12288].rearrange("p (t d) -> p t d", t=4)
        onT = regA[:, 12288:20480].rearrange("p (c t) -> p c t", c=16)
        hidT = regA[:, 0:16384].rearrange("p (c t) -> p c t", c=32)
        QT = regA[:, 0:8192].rearrange("p (c t) -> p c t", c=16)
        gfin_bc = regA[:, 0:4096].bitcast(F32) if False else None
        def rb(i, n=512):
            return regB[:, i * 512:i * 512 + n]
        tmpA, tmpB, bcum, eb, enb = rb(0), rb(1), rb(2), rb(3), rb(4)
        kdT = regB[:, 5 * 512:5 * 512 + 256].bitcast(BF16)
        etmp = [rb(6), rb(7)]
        sg = [rb(8), rb(9)]
        vbf = [regB[:, 10 * 512:10 * 512 + 256].bitcast(BF16), regB[:, 10 * 512 + 256:11 * 512].bitcast(BF16)]
        atT = regB[:, 11 * 512:11 * 512 + 128].bitcast(BF16).rearrange("p (a i) -> p a i", a=2)
        ontm = regB[:, 12 * 512:13 * 512].bitcast(BF16).rearrange("p (a i) -> p a i", a=2)
        rT = regB[0:16, 5 * 512 + 256:5 * 512 + 256 + 256]
        rT = sb("rT", [16, 512], F32)
        ltmp = [rb(6), rb(7)]

        psum = self.ps("psum", [128, 8, 512], F32)
        self.mm_i = 0
        self.tr_i = 0
        self.pb_i = 0

        def bank(i):
            return psum[:, i, :], ("ps", i)

        def next_mm():
            i = self.mm_i % 4
            self.mm_i += 1
            return bank(i)

        def next_tr():
            i = 4 + self.tr_i % 2
            self.tr_i += 1
            return psum[:, i, :].bitcast(BF16), ("ps", i)

        def next_pb():
            i = 6 + self.pb_i % 2
            self.pb_i += 1
            return bank(i)

        op, dma = self.op, self.dma

        dma("sp", ident[:], c_ident, writes=["ident"])
        dma("sp", resetm[:], c_reset, writes=["resetm"])
        dma("sp", gam[:], gamT, writes=["gam"])
        dma("sp", wg32[:], wg_d, writes=["wg32"])
        dma("sp", nbg[:], bg_d, writes=["nbg"])
        op("act", lambda e: e.mul(nbg[:], nbg[:], -1.0), reads=["nbg"], writes=["nbg"])
        w_in_v = w_in.rearrange("(kc p) n -> p kc n", p=128)
        dma("pool", wr_bf[:], w_in_v[:, :, 6144:6160], writes=["wr_bf"])
        op("dve", lambda e: e.memset(Sst[:], 0.0), writes=[("S", c) for c in range(8)])
        op("pool", lambda e: e.memset(S_bf[:], 0.0), writes=[("Sbf", c) for c in range(8)])
        op("dve", lambda e: e.memset(bsum[:], 0.0), writes=["bsum"])
        if p2:
            cmask = sb("cmask", [128, 128], BF16)
            mb = sb("mb", [128, 3, 512], BF16)
            perm = sb("perm", [128, 5, 128], F32)
            nout_bc = sb("nout_bc", [128, 512], F32)
            esink = sb("esink", [128, 32], F32)
            cval = sb("cval", [128, NCORES + 1], F32)
            ball = sb("ball", [128, NCORES, 8], F32)
            coef = sb("coef", [128, NCORES, 8], F32)
            rrun = sb("rrun", [128, 8], F32)
            cosT = sb("cosT_sb", [128, 512], F32)
            sinT = sb("sinT_sb", [128, 512], F32)
            NKV = 6
            ktd = sb("ktd", [128, NKV, 4, 128], BF16)
            vaug = sb("vaug", [128, NKV, 4, 65], BF16)
            dma("sp", cmask[:], c_cmask, writes=["cmask"])
            dma("sp", mb[:], c_mb.rearrange("a p n -> p a n"), writes=["mb"])
            dma("sp", perm[:], c_perm.rearrange("a p n -> p a n"), writes=["perm"])
            dma("sp", nout_bc[:], nout_d.to_broadcast([128, 512]), writes=["nout_bc"])
            dma("sp", esink[:], sinks_d.to_broadcast([128, 32]), writes=["esink"])
            op("act", lambda e: e.activation(esink[:], esink[:], AF.Exp), reads=["esink"], writes=["esink"])
            dma("sp", cval[:], cvalid, writes=["cval"])
            dma("sp", ball[:], B_all.rearrange("c p k -> p c k"), writes=["ball"])
            op("pool", lambda e: e.memset(vaug[:], 1.0), writes=[("vaug", i) for i in range(NKV)])
            op("dve", lambda e: e.memset(rrun[:], 0.0), writes=["rrun"])
            for cp in range(NCORES - 1, -1, -1):
                op("act", lambda e, cp=cp: e.activation(coef[:, cp, :], rrun[:], AF.Exp), reads=["rrun"], writes=[("coef", cp)])
                op("dve", lambda e, cp=cp: e.tensor_scalar(coef[:, cp, :], coef[:, cp, :], cval[:, cp:cp + 1], None, ALU.mult),
                   reads=[("coef", cp), "cval"], writes=[("coef", cp)])
                op("dve", lambda e, cp=cp: e.scalar_tensor_tensor(rrun[:], ball[:, cp, :], cval[:, cp:cp + 1], rrun[:], ALU.mult, ALU.add),
                   reads=["ball", "cval", "rrun"], writes=["rrun"])

        def hb(view, c0):
            return [view[:, 0:8, c0:c0 + 512], view[:, 8:16, c0:c0 + 512]]

        wl = []
        for st in sts:
            own_lts = [i for i, t in enumerate(st) if (not p2) or t >= 2]
            for b in (0, 1):
                if p2:
                    wl += hb(w_in_v, b * 512)
                wl += hb(w_in_v, 1024 + b * 512)
            for hd in range(4):
                wl += hb(w_in_v, 2048 + hd * 512)
                if p2:
                    wl += hb(w_in_v, 4096 + hd * 512)
            if p2:
                wo_v = a_w_out.rearrange("(kc p) n -> p kc n", p=128)
                for n in range(4):
                    wl += hb(wo_v, n * 512)
                for layer in range(2):
                    if layer == 1:
                        if not own_lts:
                            break
                        for n in range(4):
                            wl += hb(b_w_q.rearrange("(kc p) n -> p kc n", p=128), n * 512)
                        for n in range(4):
                            wl += hb(b_w_out.rearrange("(kc p) n -> p kc n", p=128), n * 512)
                    wu_v = w_up[layer].rearrange("(kc p) n -> p kc n", p=128)
                    wd_v = w_down[layer].rearrange("(kb kc p) n -> kb p kc n", kb=4, p=128)
                    for fh in range(2):
                        for fb in range(8):
                            wl += hb(wu_v, fh * 4096 + fb * 512)
                        for n in range(4):
                            for kb in range(2):
                                wl += hb(wd_v[fh * 2 + kb], n * 512)
                    if layer == 0:
                        wl += hb(w_kv.rearrange("(kc p) n -> p kc n", p=128), 0)
        self.wstream_init(wl)
        wget_block = self.wget_block

        def rms_to_xnT(st_lts, gi):
            for lt in st_lts:
                sl = 0
                ssc = small[:, lt:lt + 1]
                rsc = small[:, 8 + lt:9 + lt]
                op("act", lambda e, lt=lt, sl=sl, ssc=ssc: e.activation(xs[:, sl, :], h[:, lt, :], AF.Square, accum_out=ssc),
                   reads=[("h", lt)], writes=[("xs", sl), ("ss", lt)])
                op("dve", lambda e, ssc=ssc, rsc=rsc: e.tensor_scalar(rsc, ssc, 1.0 / D, EPS, ALU.mult, ALU.add),
                   reads=[("ss", lt)], writes=[("rs", lt)])
                op("act", lambda e, rsc=rsc: e.activation(rsc, rsc, AF.Ln), reads=[("rs", lt)], writes=[("rs", lt)])
                op("act", lambda e, rsc=rsc: e.activation(rsc, rsc, AF.Exp, scale=-0.5), reads=[("rs", lt)], writes=[("rs", lt)])
                op("act", lambda e, lt=lt, sl=sl, rsc=rsc: e.activation(xs[:, sl, :], h[:, lt, :], AF.Identity, scale=rsc),
                   reads=[("h", lt), ("rs", lt)], writes=[("xs", sl)])
                for half in range(2):
                    pt, pk = next_tr()

                    def tr(e, lt=lt, sl=sl, half=half, pt=pt):
                        for j in range(8):
                            kc = half * 8 + j
                            i = e.transpose(pt[:, j * 128:(j + 1) * 128], xs[:, sl, kc * 128:(kc + 1) * 128], ident[:])
                        return i
                    op("pe", tr, reads=[("xs", sl), "ident"], writes=[pk])
                    op("dve", lambda e, lt=lt, half=half, pt=pt: e.tensor_tensor(
                        xnT[:, half * 8:half * 8 + 8, lt * 128:(lt + 1) * 128],
                        pt.rearrange("p (c t) -> p c t", c=8),
                        gam[:, gi, half * 8:half * 8 + 8].unsqueeze(2).to_broadcast([128, 8, 128]), ALU.mult),
                       reads=[pk, "gam"], writes=[("xnT", lt)])

        def proj_fm(wblk, cols, lts_range, nchunks, evac):
            c0, c1 = lts_range[0] * 128, (lts_range[-1] + 1) * 128
            for j in range(nchunks):
                pm, pk = next_mm()

                def mm(e, j=j, pm=pm):
                    for kc in range(16):
                        wa = wblk[kc // 8][0]
                        i = e.matmul(pm[:, 0:c1 - c0], lhsT=wa[:, kc % 8, cols + j * 128:cols + (j + 1) * 128],
                                     rhs=xnT[:, kc, c0:c1], start=(kc == 0), stop=(kc == 15))
                    return i
                op("pe", mm, reads=[wblk[0][1], wblk[1][1]] + [("xnT", lt) for lt in lts_range], writes=[pk])
                evac(j, pm[:, 0:c1 - c0], pk, c0, c1)

        def proj_tm(wblk, lt, lhs, lhs_keys, ncols=512):
            pm, pk = next_mm()

            def mm(e, pm=pm):
                for kc in range(16):
                    wa = wblk[kc // 8][0]
                    i = e.matmul(pm[:, 0:ncols], lhsT=lhs[:, kc, lt * 128:(lt + 1) * 128], rhs=wa[:, kc % 8, 0:ncols],
                                 start=(kc == 0), stop=(kc == 15))
                return i
            op("pe", mm, reads=[wblk[0][1], wblk[1][1]] + list(lhs_keys), writes=[pk])
            return pm, pk

        def residual_add(lt, n, pm, pk):
            op("dve", lambda e: e.tensor_tensor(h[:, lt, n * 512:(n + 1) * 512], h[:, lt, n * 512:(n + 1) * 512], pm, ALU.add),
               reads=[pk, ("h", lt)], writes=[("h", lt)])

        gla_keys_A = [("qT", c) for c in range(8)] + [("kT", c) for c in range(8)] + [("kdec", t) for t in range(4)] + \
                     [("onT", t) for t in range(4)]
        hid_keys = [("hid", i) for i in range(32)]
        QT_keys = [("QT", lt) for lt in range(4)]
        regB_gla = ["tmpA", "tmpB", "bcum", "eb", "enb", "kdT", ("etmp", 0), ("etmp", 1), ("sg", 0), ("sg", 1),
                    ("ng", 0), ("ng", 1), ("vbf", 0), ("vbf", 1), ("atT", 0), ("atT", 1), ("ontm", 0), ("ontm", 1)]
        regB_mlp = [("rl", i) for i in range(4)]
        regB_swa = ["swaB"]

        def do_supertile(sti, st):
            nts = len(st)
            T = nts * 128
            lts = list(range(nts))
            own = [i for i, t in enumerate(st) if (not p2) or t >= 2]
            tok0 = st[0] * 128


            def dump_h():
                for lt in own:
                    g = st[lt]
                    dma("sp", out_d[(g - 2) * 128:(g - 1) * 128, :], h[:, lt, :], reads=[("h", lt)], writes=[("out", g)], key=("st", lt))
            for lt in lts:
                g = st[lt]
                dma("sp", h[:, lt, :], xin[g * 128:(g + 1) * 128, :], writes=[("h", lt)])
            dma("sp", vm_bc[:, 0:T], vmask[:, tok0:tok0 + T].to_broadcast([128, T]), writes=["vm_bc"])
            if p2:
                dma("sp", cosT[:, 0:T], cos_d[:, tok0:tok0 + T], writes=["cosT"])
                dma("sp", sinT[:, 0:T], sin_d[:, tok0:tok0 + T], writes=["sinT"])

            rms_to_xnT(lts, 0)
            pm, pk = next_mm()

            def mm_r(e, pm=pm, T=T):
                for kc in range(16):
                    i = e.matmul(pm[0:16, 0:T], lhsT=wr_bf[:, kc, :], rhs=xnT[:, kc, 0:T], start=(kc == 0), stop=(kc == 15))
                return i
            op("pe", mm_r, reads=["wr_bf"] + [("xnT", lt) for lt in lts], writes=[pk])
            op("act", lambda e, pm=pm, T=T: e.copy(rT[:, 0:T], pm[0:16, 0:T]), reads=[pk], writes=["rT"])

            if sti == 0:
                self.fence(reads=[], writes=regB_gla)
            else:
                self.fence(reads=regB_mlp + regB_swa, writes=regB_gla)
                self.fence(reads=hid_keys + QT_keys, writes=gla_keys_A)
            qk_ps = {}

            def evac_hold(kind):
                def f(j, pm, pk, c0, c1, kind=kind):
                    qk_ps[(kind, f.base + j)] = (pm, pk)
                return f

            def gate_chunk(c):
                pz, pzk = next_pb()
                op("pe", lambda e, pz=pz: e.matmul(pz[:, 0:T], lhsT=wg32[:, c * 128:(c + 1) * 128], rhs=rT[:, 0:T], start=True, stop=True),
                   reads=["wg32", "rT"], writes=[pzk])
                op("act", lambda e, pz=pz: e.activation(tmpA[:, 0:T], pz[:, 0:T], AF.Exp, bias=nbg[:, c:c + 1], scale=-1.0),
                   reads=[pzk, "nbg"], writes=["tmpA"])
                op("act", lambda e: e.activation(tmpA[:, 0:T], tmpA[:, 0:T], AF.Ln, bias=1.0), reads=["tmpA"], writes=["tmpA"])
                op("dve", lambda e: e.scalar_tensor_tensor(tmpB[:, 0:T], tmpA[:, 0:T], -1.0 / 16.0, vm_bc[:, 0:T], ALU.mult, ALU.mult),
                   reads=["tmpA", "vm_bc"], writes=["tmpB"])
                op("dve", lambda e: e.tensor_tensor_scan(bcum[:, 0:T], resetm[:, 0:T], tmpB[:, 0:T], 0.0, ALU.mult, ALU.add),
                   reads=["tmpB", "resetm"], writes=["bcum"])
                if p2:
                    op("act", lambda e: e.activation(eb[:, 0:T], bcum[:, 0:T], AF.Exp), reads=["bcum"], writes=["eb"])
                op("act", lambda e: e.activation(enb[:, 0:T], bcum[:, 0:T], AF.Exp, scale=-1.0), reads=["bcum"], writes=["enb"])
                bl = bcum[:, 0:T].rearrange("p (t i) -> p t i", i=128)[:, :, 127:128]
                op("act", lambda e: e.activation(eblast[:, c, 0:nts].unsqueeze(2), bl, AF.Exp), reads=["bcum"], writes=[("ebl", c)])
                if not p2:
                    for t in range(nts):
                        op("dve", lambda e, t=t: e.tensor_tensor(bsum[:, c:c + 1], bsum[:, c:c + 1], bl[:, t, :], ALU.add),
                           reads=["bcum", "bsum"], writes=["bsum"])

            def qk_evac(c):
                if p2:
                    pq, pqk = qk_ps.pop(("q", c))
                    op("dve", lambda e: e.scalar_tensor_tensor(qT[:, c, 0:T], pq, 1.0 / 16.0, eb[:, 0:T], ALU.mult, ALU.mult),
                       reads=[pqk, "eb"], writes=[("qT", c)])
                pkk, pkkk = qk_ps.pop(("k", c))
                op("dve", lambda e: e.tensor_tensor(tmpA[:, 0:T], pkk, enb[:, 0:T], ALU.mult), reads=[pkkk, "enb"], writes=["tmpA"])
                if p2:
                    op("act", lambda e: e.copy(kT[:, c, 0:T], tmpA[:, 0:T]), reads=["tmpA"], writes=[("kT", c)])
                op("dve", lambda e: e.tensor_tensor(kdT[:, 0:T].rearrange("p (t i) -> p t i", i=128),
                                                    tmpA[:, 0:T].rearrange("p (t i) -> p t i", i=128),
                                                    eblast[:, c, 0:nts].unsqueeze(2).to_broadcast([128, nts, 128]), ALU.mult),
                   reads=["tmpA", ("ebl", c)], writes=["kdT"])
                pt, ptk = next_tr()

                def tr(e, pt=pt):
                    for t in range(nts):
                        i = e.transpose(pt[:, t * 128:(t + 1) * 128], kdT[:, t * 128:(t + 1) * 128], ident[:])
                    return i
                op("pe", tr, reads=["kdT", "ident"], writes=[ptk])
                op("act", lambda e, pt=pt: e.copy(kdec[:, 0:nts, c * 128:(c + 1) * 128], pt[:, 0:T].rearrange("p (t d) -> p t d", d=128)),
                   reads=[ptk], writes=[("kdec", t) for t in range(nts)])

            def proj_fm_chunk(wblk, col0, c0, c1):
                pm, pk = next_mm()

                def mm(e, pm=pm):
                    for kc in range(16):
                        wa = wblk[kc // 8][0]
                        i = e.matmul(pm[:, 0:c1 - c0], lhsT=wa[:, kc % 8, col0:col0 + 128], rhs=xnT[:, kc, c0:c1],
                                     start=(kc == 0), stop=(kc == 15))
                    return i
                op("pe", mm, reads=[wblk[0][1], wblk[1][1]] + [("xnT", lt) for lt in range(c0 // 128, c1 // 128)], writes=[pk])
                return pm[:, 0:c1 - c0], pk

            for b in range(2):
                wqb = wget_block() if p2 else None
                wkb = wget_block()
                for j in range(4):
                    c = b * 4 + j
                    gate_chunk(c)
                    if p2:
                        qk_ps[("q", c)] = proj_fm_chunk(wqb, j * 128, 0, T)
                    qk_ps[("k", c)] = proj_fm_chunk(wkb, j * 128, 0, T)
                    qk_evac(c)
                self.wrel(4 if p2 else 2)

            tog = 0
            for hd in range(4):
                wv = wget_block()
                wgt = wget_block() if p2 else None
                for lt in lts:
                    g = st[lt]
                    i2 = tog % 2
                    tog += 1
                    tc0, tc1 = lt * 128, (lt + 1) * 128
                    pv, pvk = proj_tm(wv, lt, xnT, [("xnT", lt)])
                    op("act", lambda e, pv=pv, i2=i2: e.copy(vbf[i2][:, :], pv), reads=[pvk], writes=[("vbf", i2)])
                    if p2:
                        pg, pgk = proj_tm(wgt, lt, xnT, [("xnT", lt)])
                        op("act", lambda e, pg=pg, i2=i2: e.activation(etmp[i2], pg, AF.Exp, scale=-1.0), reads=[pgk], writes=[("etmp", i2)])
                        op("dve", lambda e, i2=i2: e.tensor_scalar(etmp[i2], etmp[i2], 1.0, None, ALU.add), reads=[("etmp", i2)], writes=[("etmp", i2)])
                        op("dve", lambda e, i2=i2: e.reciprocal(etmp[i2], etmp[i2]), reads=[("etmp", i2)], writes=[("etmp", i2)])
                        op("dve", lambda e, pg=pg, i2=i2: e.tensor_tensor(sg[i2], pg, etmp[i2], ALU.mult), reads=[pgk, ("etmp", i2)], writes=[("sg", i2)])
                        op("pool", lambda e, i2=i2: e.tensor_tensor(sg[i2], sg[i2], nout_bc[:], ALU.mult), reads=[("sg", i2), "nout_bc"], writes=[("sg", i2)])
                        pa, pak = next_pb()

                        def mm_a(e, pa=pa, hd=hd, tc0=tc0, tc1=tc1):
                            for dc in range(2):
                                c = 2 * hd + dc
                                i = e.matmul(pa[:, 0:128], lhsT=kT[:, c, tc0:tc1], rhs=qT[:, c, tc0:tc1], start=(dc == 0), stop=(dc == 1))
                            return i
                        op("pe", mm_a, reads=[("kT", 2 * hd), ("kT", 2 * hd + 1), ("qT", 2 * hd), ("qT", 2 * hd + 1)], writes=[pak])
                        op("dve", lambda e, pa=pa, i2=i2: e.tensor_tensor(atT[:, i2, :], pa[:, 0:128], cmask[:], ALU.mult),
                           reads=[pak, "cmask"], writes=[("atT", i2)])
                        po, pok = next_mm()

                        def mm_o(e, po=po, hd=hd, i2=i2, tc0=tc0, tc1=tc1):
                            e.matmul(po, lhsT=atT[:, i2, :], rhs=vbf[i2][:, :], start=True, stop=False)
                            for dc in range(2):
                                c = 2 * hd + dc
                                i = e.matmul(po, lhsT=qT[:, c, tc0:tc1], rhs=S_bf[:, c, :], start=False, stop=(dc == 1))
                            return i
                        op("pe", mm_o, reads=[("atT", i2), ("vbf", i2), ("qT", 2 * hd), ("qT", 2 * hd + 1), ("Sbf", 2 * hd), ("Sbf", 2 * hd + 1)],
                           writes=[pok])
                    for dc in range(2):
                        c = 2 * hd + dc
                        pu, puk = next_mm()
                        op("pe", lambda e, pu=pu, c=c, lt=lt, i2=i2: e.matmul(pu, lhsT=kdec[:, lt, c * 128:(c + 1) * 128], rhs=vbf[i2][:, :], start=True, stop=True),
                           reads=[("kdec", lt), ("vbf", i2)], writes=[puk])
                        op("dve", lambda e, pu=pu, c=c, lt=lt: e.scalar_tensor_tensor(Sst[:, c, :], Sst[:, c, :], eblast[:, c, lt:lt + 1], pu, ALU.mult, ALU.add),
                           reads=[puk, ("S", c), ("ebl", c)], writes=[("S", c)])
                        if p2:
                            op("act", lambda e, c=c: e.copy(S_bf[:, c, :], Sst[:, c, :]), reads=[("S", c)], writes=[("Sbf", c)])
                    if p2:
                        ssc = small[:, 16 + i2:17 + i2]
                        rsc = small[:, 18 + i2:19 + i2]
                        sk, rk = ("oss", i2), ("ors", i2)
                        op("act", lambda e, po=po, i2=i2, ssc=ssc: e.activation(etmp[i2], po, AF.Square, accum_out=ssc), reads=[pok], writes=[("etmp", i2), sk])
                        op("dve", lambda e, ssc=ssc, rsc=rsc: e.tensor_scalar(rsc, ssc, 1.0 / 512.0, EPS, ALU.mult, ALU.add), reads=[sk], writes=[rk])
                        op("act", lambda e, rsc=rsc: e.activation(rsc, rsc, AF.Ln), reads=[rk], writes=[rk])
                        op("act", lambda e, rsc=rsc: e.activation(rsc, rsc, AF.Exp, scale=-0.5), reads=[rk], writes=[rk])
                        op("dve", lambda e, po=po, i2=i2, rsc=rsc: e.scalar_tensor_tensor(ontm[:, i2, :], po, rsc, sg[i2], ALU.mult, ALU.mult),
                           reads=[pok, rk, ("sg", i2)], writes=[("ontm", i2)])
                        pt, ptk = next_tr()

                        def tr_o(e, pt=pt, i2=i2):
                            for j in range(4):
                                i = e.transpose(pt[:, j * 128:(j + 1) * 128], ontm[:, i2, j * 128:(j + 1) * 128], ident[:])
                            return i
                        op("pe", tr_o, reads=[("ontm", i2), "ident"], writes=[ptk])
                        op("act", lambda e, pt=pt, hd=hd, tc0=tc0, tc1=tc1: e.copy(onT[:, hd * 4:(hd + 1) * 4, tc0:tc1], pt[:, 0:512].rearrange("p (c t) -> p c t", c=4)),
                           reads=[ptk], writes=[("onT", lt)])
                        if g == 0:
                            for dc in range(2):
                                c = 2 * hd + dc
                                op("dve", lambda e, c=c: e.tensor_scalar(Sst[:, c, :], Sst[:, c, :], cval[:, NCORES:NCORES + 1], None, ALU.mult),
                                   reads=[("S", c), "cval"], writes=[("S", c)])
                                for cp in range(NCORES - 1, -1, -1):
                                    j2 = cp % 2
                                    dma("sp", ltmp[j2], L_all[cp, :, c, :], writes=[("etmp", j2)])
                                    op("dve", lambda e, c=c, cp=cp, j2=j2: e.scalar_tensor_tensor(Sst[:, c, :], ltmp[j2], coef[:, cp, c:c + 1], Sst[:, c, :], ALU.mult, ALU.add),
                                       reads=[("etmp", j2), ("coef", cp), ("S", c)], writes=[("S", c)])
                                op("act", lambda e, c=c: e.copy(S_bf[:, c, :], Sst[:, c, :]), reads=[("S", c)], writes=[("Sbf", c)])

                self.wrel(4 if p2 else 2)

            if not p2:
                return
            if self.stop <= 2:
                dump_h()
                return

            for n in range(4):
                wo = wget_block()
                for lt in lts:
                    pm, pk = proj_tm(wo, lt, onT, [("onT", lt)])
                    residual_add(lt, n, pm, pk)
                self.wrel(2)

            def mlp(layer, mlts):
                c0, c1 = mlts[0] * 128, (mlts[-1] + 1) * 128
                rms_to_xnT(mlts, 1 if layer == 0 else 4)
                self.fence(reads=gla_keys_A + QT_keys, writes=hid_keys)
                self.fence(reads=regB_gla + regB_swa, writes=regB_mlp)
                ri = 0
                for fh in range(2):
                    for fb in range(8):
                        wu = wget_block()
                        for j in range(4):
                            fc = fb * 4 + j
                            pm, pk = proj_fm_chunk(wu, j * 128, c0, c1)
                            r = ri % 4
                            ri += 1
                            op("act", lambda e, pm=pm, r=r: e.activation(rb(r)[:, 0:c1 - c0], pm, AF.Relu), reads=[pk], writes=[("rl", r)])
                            op("pool", lambda e, r=r, fc=fc: e.tensor_tensor(hidT[:, fc, c0:c1], rb(r)[:, 0:c1 - c0], rb(r)[:, 0:c1 - c0], ALU.mult),
                               reads=[("rl", r)], writes=[("hid", fc)])
                        self.wrel(2)
                    for n in range(4):
                        base = ((fh * 4 + n) % 2) * 4
                        wds = [wget_block() for _ in range(2)] if False else None
                        for kb in range(2):
                            wd = wget_block()
                            for hh in range(2):
                                wa, wk = wd[hh]
                                for lt in mlts:
                                    pm, pk = bank(base + lt)

                                    def mm(e, pm=pm, wa=wa, lt=lt, kb=kb, hh=hh):
                                        for kc in range(8):
                                            fc = kb * 16 + hh * 8 + kc
                                            i = e.matmul(pm, lhsT=hidT[:, fc, lt * 128:(lt + 1) * 128], rhs=wa[:, kc, :],
                                                         start=(kb == 0 and hh == 0 and kc == 0), stop=(kb == 1 and hh == 1 and kc == 7))
                                        return i
                                    op("pe", mm, reads=[wk] + [("hid", kb * 16 + hh * 8 + kc) for kc in range(8)], writes=[pk])
                            self.wrel(2)
                        for lt in mlts:
                            pm, pk = bank(base + lt)
                            residual_add(lt, n, pm, pk)

            if self.stop <= 3:
                dump_h()
                return
            mlp(0, lts)
            if self.stop <= 4:
                dump_h()
                return

            rms_to_xnT(lts, 2)
            self.fence(reads=regB_mlp, writes=regB_swa)
            wkv = wget_block()
            ksb = regB[:, 4 * 512:6 * 512].rearrange("p (a t) -> p a t", a=2)
            for kc2 in range(2):
                pm, pk = proj_fm_chunk(wkv, kc2 * 128, 0, T)
                op("act", lambda e, pm=pm, kc2=kc2: e.copy(ksb[:, kc2, 0:T], pm), reads=[pk], writes=["swaB"])

            def kvslot(g):
                return 0 if g == 0 else 1 + ((g - 1) % (NKV - 1))
            for gkv in range(4):
                src = ksb[:, gkv // 2, 0:T]
                pd, pdk = next_pb()
                op("pe", lambda e, pd=pd, src=src, gkv=gkv: e.matmul(pd[:, 0:T], lhsT=perm[:, 1 + gkv % 2, :], rhs=src, start=True, stop=True),
                   reads=["swaB", "perm"], writes=[pdk])
                pr, prk = next_pb()
                op("pe", lambda e, pr=pr, src=src, gkv=gkv: e.matmul(pr[:, 0:T], lhsT=perm[:, 3 + gkv % 2, :], rhs=src, start=True, stop=True),
                   reads=["swaB", "perm"], writes=[prk])
                op("dve", lambda e, pd=pd: e.tensor_tensor(rb(2)[:, 0:T], pd[:, 0:T], cosT[:, 0:T], ALU.mult), reads=[pdk, "cosT"], writes=["swaB"])
                op("dve", lambda e, pr=pr: e.tensor_tensor(rb(3)[:, 0:T], pr[:, 0:T], sinT[:, 0:T], ALU.mult), reads=[prk, "sinT"], writes=["swaB"])
                for lt in lts:
                    sl = kvslot(st[lt])
                    op("pool", lambda e, lt=lt, sl=sl, gkv=gkv: e.tensor_tensor(ktd[:, sl, gkv, :], rb(2)[:, lt * 128:(lt + 1) * 128], rb(3)[:, lt * 128:(lt + 1) * 128], ALU.add),
                       reads=["swaB"], writes=[("ktd", sl)])
            for lt in lts:
                sl = kvslot(st[lt])
                pm, pk = next_mm()

                def mm_v(e, pm=pm, lt=lt):
                    for kc in range(16):
                        wa = wkv[kc // 8][0]
                        i = e.matmul(pm[:, 0:256], lhsT=xnT[:, kc, lt * 128:(lt + 1) * 128], rhs=wa[:, kc % 8, 256:512], start=(kc == 0), stop=(kc == 15))
                    return i
                op("pe", mm_v, reads=[wkv[0][1], wkv[1][1], ("xnT", lt)], writes=[pk])
                op("act", lambda e, pm=pm, sl=sl: e.copy(vaug[:, sl, :, 0:64], pm[:, 0:256].rearrange("p (g d) -> p g d", g=4)),
                   reads=[pk], writes=[("vaug", sl)])
            self.wrel(2)

            if not own:
                return
            if self.stop <= 5:
                dump_h()
                return
            c0, c1 = own[0] * 128, (own[-1] + 1) * 128
            Tn = c1 - c0
            rms_to_xnT(own, 3)
            self.fence(reads=hid_keys + gla_keys_A, writes=QT_keys)
            qi = 0
            wqb_v = None
            for n in range(4):
                wq = wget_block()
                for j in range(4):
                    c = n * 4 + j
                    pm, pk = proj_fm_chunk(wq, j * 128, c0, c1)
                    qs = rb(qi % 2)
                    t1 = rb(2 + qi % 2)
                    qsk, t1k = ("qsb", qi % 2), ("rt1", qi % 2)
                    qi += 1
                    op("act", lambda e, pm=pm, qs=qs: e.copy(qs[:, 0:Tn], pm), reads=[pk, "swaB"], writes=[qsk])
                    pr, prk = next_pb()
                    op("pe", lambda e, pr=pr, qs=qs: e.matmul(pr[:, 0:Tn], lhsT=perm[:, 0, :], rhs=qs[:, 0:Tn], start=True, stop=True),
                       reads=[qsk, "perm"], writes=[prk])
                    op("pool", lambda e, qs=qs: e.tensor_tensor(qs[:, 0:Tn], qs[:, 0:Tn], cosT[:, c0:c1], ALU.mult), reads=[qsk, "cosT", prk], writes=[qsk])
                    op("dve", lambda e, pr=pr, t1=t1: e.tensor_tensor(t1[:, 0:Tn], pr[:, 0:Tn], sinT[:, c0:c1], ALU.mult), reads=[prk, "sinT", "swaB"], writes=[t1k])
                    op("pool", lambda e, qs=qs, t1=t1, c=c: e.tensor_tensor(QT[:, c, c0:c1], qs[:, 0:Tn], t1[:, 0:Tn], ALU.add),
                       reads=[qsk, t1k], writes=[("QT", lt) for lt in own])
                self.wrel(2)
            if self.stop <= 5.2:
                dump_h()
                return
            ptv = regB[:, 6 * 512:9 * 512].bitcast(BF16).rearrange("p (w a n) -> p w a n", w=3, a=2)
            dn = regB[:, 9 * 512:9 * 512 + 16].rearrange("p (a n) -> p a n", a=2)
            for lt in own:
                g = st[lt]
                sl_c, sl_p = kvslot(g), kvslot(g - 1)
                tc0, tc1 = lt * 128, (lt + 1) * 128
                xsl = 0
                for gkv in range(4):
                    for par in range(2):
                        pr0 = par * 64
                        for wi, (slk, mbi) in enumerate(((sl_c, 1), (sl_p, 2 if g == 2 else 0))):
                            pz, pzk = next_pb()

                            def mm_s(e, pz=pz, slk=slk, mbi=mbi, gkv=gkv, pr0=pr0, tc0=tc0, tc1=tc1):
                                e.matmul(pz, lhsT=ident[:], rhs=mb[:, mbi, :], start=True, stop=False)
                                for i4 in range(4):
                                    i = e.matmul(pz[:, i4 * 128:(i4 + 1) * 128], lhsT=ktd[pr0:pr0 + 64, slk, gkv, :],
                                                 rhs=QT[pr0:pr0 + 64, 4 * gkv + i4, tc0:tc1], start=False, stop=(i4 == 3))
                                return i
                            op("pe", mm_s, reads=["ident", "mb", ("ktd", slk), ("QT", lt)], writes=[pzk])
                            op("act", lambda e, pz=pz, wi=wi, par=par: e.activation(ptv[:, wi, par, :], pz, AF.Exp, scale=0.125),
                               reads=[pzk, "swaB"], writes=[("pt", wi, par)])
                        pz, pzk = next_pb()

                        def mm_sm(e, pz=pz, gkv=gkv, pr0=pr0, tc0=tc0, tc1=tc1):
                            for i4 in range(4):
                                i = e.matmul(pz[0:16, i4 * 128:(i4 + 1) * 128], lhsT=ktd[pr0:pr0 + 64, 0, gkv, 0:16],
                                             rhs=QT[pr0:pr0 + 64, 4 * gkv + i4, tc0:tc1], start=True, stop=True)
                            return i
                        op("pe", mm_sm, reads=[("ktd", 0), ("QT", lt)], writes=[pzk])
                        op("act", lambda e, pz=pz, par=par: e.activation(ptv[0:16, 2, par, :], pz[0:16, :], AF.Exp, scale=0.125),
                           reads=[pzk, "swaB"], writes=[("pt", 2, par)])
                        if self.stop <= 5.4:
                            continue
                        po, pok = next_mm()

                        def mm_pv(e, po=po, par=par, gkv=gkv, sl_c=sl_c, sl_p=sl_p):
                            for i4 in range(4):
                                oo = po[:, i4 * 65:(i4 + 1) * 65]
                                e.matmul(oo, lhsT=ptv[0:16, 2, par, i4 * 128:(i4 + 1) * 128], rhs=vaug[0:16, 0, gkv, :], start=True, stop=False)
                                e.matmul(oo, lhsT=ptv[:, 1, par, i4 * 128:(i4 + 1) * 128], rhs=vaug[:, sl_p, gkv, :], start=False, stop=False)
                                i = e.matmul(oo, lhsT=ptv[:, 0, par, i4 * 128:(i4 + 1) * 128], rhs=vaug[:, sl_c, gkv, :], start=False, stop=True)
                            return i
                        op("pe", mm_pv, reads=[("pt", 0, par), ("pt", 1, par), ("pt", 2, par), ("vaug", 0), ("vaug", sl_p), ("vaug", sl_c)], writes=[pok])
                        if self.stop <= 5.5:
                            continue
                        pov = po[:, 0:260].rearrange("p (h d) -> p h d", h=4)
                        es = esink[:, 8 * gkv:8 * gkv + 8].rearrange("p (i r) -> p i r", r=2)[:, :, par:par + 1]
                        op("dve", lambda e, pov=pov, par=par, es=es: e.tensor_tensor(dn[:, par, 0:4].unsqueeze(2), pov[:, :, 64:65], es, ALU.add),
                           reads=[pok, "esink", "swaB"], writes=[("dn", par)])
                        op("dve", lambda e, par=par: e.reciprocal(dn[:, par, 4:8], dn[:, par, 0:4]), reads=[("dn", par)], writes=[("dn", par)])
                        op("dve", lambda e, pov=pov, par=par, gkv=gkv, xsl=xsl: e.tensor_tensor(
                            xs[:, xsl, gkv * 512:(gkv + 1) * 512].rearrange("p (i r d) -> p i r d", r=2, d=64)[:, :, par, :], pov[:, :, 0:64],
                            dn[:, par, 4:8].unsqueeze(2).to_broadcast([128, 4, 64]), ALU.mult),
                           reads=[pok, ("dn", par)], writes=[("xs", xsl)])
                if self.stop <= 5.6:
                    continue
                for half in range(2):
                    pt, ptk = next_tr()

                    def tr(e, pt=pt, xsl=xsl, half=half):
                        for j in range(8):
                            kc = half * 8 + j
                            i = e.transpose(pt[:, j * 128:(j + 1) * 128], xs[:, xsl, kc * 128:(kc + 1) * 128], ident[:])
                        return i
                    op("pe", tr, reads=[("xs", xsl), "ident"], writes=[ptk])
                    op("act", lambda e, pt=pt, half=half, tc0=tc0, tc1=tc1: e.copy(onT[:, half * 8:half * 8 + 8, tc0:tc1], pt.rearrange("p (c t) -> p c t", c=8)),
                       reads=[ptk], writes=[("onT", lt)])
            if self.stop <= 5.7:
                dump_h()
                return
            for n in range(4):
                wo = wget_block()
                for lt in own:
                    pm, pk = proj_tm(wo, lt, onT, [("onT", lt)])
                    residual_add(lt, n, pm, pk)
                self.wrel(2)

            if self.stop <= 6:
                dump_h()
                return
            mlp(1, own)
            if self.stop <= 7:
                dump_h()
                return

            gfin = regA[:, 0:4096].bitcast(F32)
            self.fence(reads=hid_keys, writes=["gfin"])
            dma("sp", gfin, gfin_d.to_broadcast([128, D]), writes=["gfin"])
            for lt in own:
                g = st[lt]
                ssc = small[:, lt:lt + 1]
                rsc = small[:, 8 + lt:9 + lt]
                sl = 0
                op("act", lambda e, lt=lt, sl=sl, ssc=ssc: e.activation(xs[:, sl, :], h[:, lt, :], AF.Square, accum_out=ssc),
                   reads=[("h", lt)], writes=[("xs", sl), ("ss", lt)])
                op("dve", lambda e, ssc=ssc, rsc=rsc: e.tensor_scalar(rsc, ssc, 1.0 / D, EPS, ALU.mult, ALU.add), reads=[("ss", lt)], writes=[("rs", lt)])
                op("act", lambda e, rsc=rsc: e.activation(rsc, rsc, AF.Ln), reads=[("rs", lt)], writes=[("rs", lt)])
                op("act", lambda e, rsc=rsc: e.activation(rsc, rsc, AF.Exp, scale=-0.5), reads=[("rs", lt)], writes=[("rs", lt)])
                op("dve", lambda e, lt=lt, rsc=rsc: e.scalar_tensor_tensor(h[:, lt, :], h[:, lt, :], rsc, gfin, ALU.mult, ALU.mult),
                   reads=[("h", lt), ("rs", lt), "gfin"], writes=[("h", lt)])
                dma("sp", out_d[(g - 2) * 128:(g - 1) * 128, :], h[:, lt, :], reads=[("h", lt)], writes=[("out", g)], key=("st", lt))
            self.fence(reads=["gfin"], writes=gla_keys_A)

        for sti_, st_ in enumerate(sts):
            do_supertile(sti_, st_)

        if not p2:
            for c in range(8):
                dma("sp", L_out[:, c, :], Sst[:, c, :], reads=[("S", c)], writes=[("Lout", c)], key="Lout")
            dma("sp", B_out, bsum[:], reads=["bsum"], writes=["Bout"], key="Lout")
            fin_reads = [("Lout", c) for c in range(8)] + ["Bout"]
        else:
            fin_reads = [("out", g) for g in range(2, ntiles)]
        S.add("sp", lambda e: e.nop(), reads=fin_reads, writes=["fin"])
        S.finalize(nc, self.stack)
        S.emit(nc)
        self.stack.close()
        return nc


def _consts():
    bf = ml_dtypes.bfloat16
    ident = np.eye(128, dtype=np.float32).astype(bf)
    reset = np.ones((128, 512), np.float32)
    reset[:, 0::128] = 0.0
    j = np.arange(128)[:, None]
    i = np.arange(128)[None, :]
    cmask = (j <= i).astype(np.float32).astype(bf)
    NEG = -30000.0
    mb_prev = np.where(j > i, 0.0, NEG).astype(np.float32)
    mb_cur = np.where(j <= i, 0.0, NEG).astype(np.float32)
    mb_none = np.full((128, 128), NEG, np.float32)
    perm = np.zeros((5, 128, 128), np.float32)
    for m in range(128):
        blk, ii = m // 64, m % 64
        if ii < 32:
            perm[0, blk * 64 + ii + 32, m] = -1.0
        else:
            perm[0, blk * 64 + ii - 32, m] = 1.0
        for par in range(2):
            perm[1 + par, par * 64 + ii, m] = 1.0
            if ii < 32:
                perm[3 + par, par * 64 + ii + 32, m] = -1.0
            else:
                perm[3 + par, par * 64 + ii - 32, m] = 1.0
    return ident, reset, cmask, mb_prev, mb_cur, mb_none, perm


def _rope_tables(pos):
    half = 32
    inv_freq = (10000.0 ** (-np.arange(half, dtype=np.float32) / half)).astype(np.float32)
    ang = pos.astype(np.float32)[None, :] * inv_freq[:, None]
    cos = np.cos(ang).astype(np.float32)
    sin = np.sin(ang).astype(np.float32)
    return np.tile(cos, (4, 1)), np.tile(sin, (4, 1))


_NC_CACHE = {}


_STOP = 99


def _get_nc(mode, nto):
    k = (mode, nto, _STOP)
    if k not in _NC_CACHE:
        _NC_CACHE[k] = Builder(mode, nt_own=nto, nslot=4, stop=_STOP).build()
    return _NC_CACHE[k]


def kernel(x, meta_tokens, norm_mix, norm_mlp, w_mlp_up, w_mlp_down, a_w_in, a_w_gate_up, a_b_gate,
           a_norm_out, a_w_out, kv_norm, w_kv, b_w_q, b_sinks, b_w_out, norm_final, _nto=NT_OWN, _ncores=NCORES):
    f32 = np.float32
    bf = ml_dtypes.bfloat16
    x = np.asarray(x, f32)[0]
    nto = _nto
    tpc = nto * 128
    ident, reset, cmask, mb_prev, mb_cur, mb_none, perm = _consts()
    meta_tile = np.zeros((128, D), f32)
    meta_tile[:NMETA] = np.asarray(meta_tokens, f32)
    zero_tile = np.zeros((128, D), f32)
    gam = np.stack([np.asarray(norm_mix, f32)[0], np.asarray(norm_mlp, f32)[0], np.asarray(kv_norm, f32),
                    np.asarray(norm_mix, f32)[1], np.asarray(norm_mlp, f32)[1]], 0)
    gamT = np.ascontiguousarray(gam.reshape(5, 16, 128).transpose(2, 0, 1))
    bgT = np.ascontiguousarray(np.asarray(a_b_gate, f32)[0].reshape(8, 128).T)
    w_in = np.ascontiguousarray(np.asarray(a_w_in, f32)[0])
    wg = np.ascontiguousarray(np.asarray(a_w_gate_up, f32)[0])
    common = {"w_in": w_in, "gamT": gamT, "wg": wg, "bgT": bgT, "c_ident": ident, "c_reset": reset}

    nc1 = _get_nc("p1", nto)
    in1 = []
    for c in range(_ncores):
        s = c * tpc
        first = meta_tile if c == 0 else x[s - 128:s]
        xin = np.concatenate([first, x[s:s + tpc - 128]], 0)
        vm = np.ones((1, nto * 128), f32)
        if c == 0:
            vm[0, NMETA:128] = 0.0
        in1.append(dict(common, xin=np.ascontiguousarray(xin), vmask=vm))
    r1 = run_bass_kernel_spmd(nc1, in1, core_ids=list(range(_ncores)))
    L_all = np.stack([r["L_out"] for r in r1.results], 0)
    B_all = np.stack([r["B_out"] for r in r1.results], 0)
    if _ncores < NCORES:
        L_all = np.concatenate([L_all, np.zeros((NCORES - _ncores,) + L_all.shape[1:], f32)], 0)
        B_all = np.concatenate([B_all, np.zeros((NCORES - _ncores,) + B_all.shape[1:], f32)], 0)

    nc2 = _get_nc("p2", nto)
    com2 = dict(common,
                w_up=np.asarray(w_mlp_up, f32), w_down=np.asarray(w_mlp_down, f32),
                a_w_out=np.ascontiguousarray(np.asarray(a_w_out, f32)[0]), w_kv=np.asarray(w_kv, f32),
                b_w_q=np.ascontiguousarray(np.asarray(b_w_q, f32)[0]), b_w_out=np.ascontiguousarray(np.asarray(b_w_out, f32)[0]),
                nout=np.asarray(a_norm_out, f32).reshape(1, 512), gfin=np.asarray(norm_final, f32).reshape(1, D),
                sinks=np.asarray(b_sinks, f32).reshape(1, 32), c_cmask=cmask, c_perm=perm,
                L_all=np.ascontiguousarray(L_all), B_all=np.ascontiguousarray(B_all))
    in2 = []
    for c in range(_ncores):
        s = c * tpc
        halo = zero_tile if c == 0 else x[s - 128:s]
        xin = np.concatenate([meta_tile, halo, x[s:s + tpc]], 0)
        ntok = xin.shape[0]
        vm = np.ones((1, ntok), f32)
        vm[0, NMETA:128] = 0.0
        if c == 0:
            vm[0, 128:256] = 0.0
        pos = np.zeros(ntok, np.int64)
        pos[:NMETA] = np.arange(NMETA)
        pos[128:] = NMETA + (s - 128) + np.arange(ntok - 128)
        pos = np.maximum(pos, 0)
        cosT, sinT = _rope_tables(pos)
        mbp0 = mb_none if c == 0 else mb_prev
        c_mb = np.stack([np.tile(mb_prev, (1, 4)), np.tile(mb_cur, (1, 4)), np.tile(mbp0, (1, 4))], 0).astype(bf)
        cv = np.zeros((128, NCORES + 1), f32)
        cv[:, :c] = 1.0
        cv[:, NCORES] = 1.0 if c == 0 else 0.0
        in2.append(dict(com2, xin=np.ascontiguousarray(xin), vmask=vm, cosT=cosT, sinT=sinT, c_mb=c_mb, cvalid=cv))
    r2 = run_bass_kernel_spmd(nc2, in2, core_ids=list(range(_ncores)))
    out = np.concatenate([r["out"] for r in r2.results], 0)
    return out[None].astype(f32)
```
